# Optimizing a Trainium2 kernel written in Bass

```python
import math
import jax, jax.numpy as jnp
from jax import lax
import numpy as np

D_MODEL = 1024
BATCH = 2
SEQ = 16384
DEPTH = 2

HEAD_DIM = 64
DILATED_GROUPS = ((128, 1), (512, 4), (2048, 16))
N_DIL = len(DILATED_GROUPS)
HEADS_PER_GROUP = 8
N_ATTN_HEADS = N_DIL * HEADS_PER_GROUP
ATTN_WIDTH = N_ATTN_HEADS * HEAD_DIM
ATTN_OUT_WIDTH = HEADS_PER_GROUP * HEAD_DIM
ATTN_BLOCK = 128
N_REL_BUCKETS = 32
REL_MAX_DISTANCE = 2048

D_INNER = 2 * D_MODEL
SSM_HEAD_DIM = 64
N_SSM_HEADS = D_INNER // SSM_HEAD_DIM
N_SSM_GROUPS = 4
HEADS_PER_SSM_GROUP = N_SSM_HEADS // N_SSM_GROUPS
D_STATE = 128
CONV_WIDTH = 4
SSD_CHUNK = 128
XBC_WIDTH = D_INNER + 2 * N_SSM_GROUPS * D_STATE

N_BRANCHES = 2
D_FF = -(-8 * D_MODEL // (3 * 256)) * 256
IN_PROJ_WIDTH = 3 * ATTN_WIDTH + D_INNER + XBC_WIDTH + N_SSM_HEADS + N_BRANCHES * D_MODEL
EPS = 1e-6

kernel_name = "hybrid_dilated_attn_ssd_gated_block"


def rmsnorm(x, w):
    xf = x.astype(jnp.float32)
    y = xf * lax.rsqrt(jnp.mean(xf * xf, axis=-1, keepdims=True) + EPS)
    return (y * w.astype(jnp.float32)).astype(x.dtype)


def t5_causal_bucket(dist):
    max_exact = N_REL_BUCKETS // 2
    d_f = jnp.maximum(dist, 1).astype(jnp.float32)
    large = max_exact + (jnp.log(d_f / max_exact) / math.log(REL_MAX_DISTANCE / max_exact)
                         * (N_REL_BUCKETS - max_exact)).astype(jnp.int32)
    large = jnp.minimum(large, N_REL_BUCKETS - 1)
    return jnp.where(dist < max_exact, dist, large)


def rel_bias_block(rel_bias_g, dilation, n_steps):
    qi = jnp.arange(ATTN_BLOCK)[:, None]
    kj = jnp.arange(2 * ATTN_BLOCK)[None, :]
    steps = jnp.clip(qi + ATTN_BLOCK - kj, 0, n_steps)
    bucket = t5_causal_bucket(steps * dilation)
    return jnp.transpose(rel_bias_g[bucket], (2, 0, 1)).astype(jnp.float32)


def dilated_window_attention(q, k, v, bias, dilation, n_steps):
    b, s, h, dh = q.shape
    seg = s // dilation
    nb = -(-seg // ATTN_BLOCK)
    segp = nb * ATTN_BLOCK

    def to_blocks(t):
        t = t.reshape(b, seg, dilation, h, dh).transpose(0, 2, 1, 3, 4).reshape(b * dilation, seg, h, dh)
        t = jnp.pad(t, ((0, 0), (0, segp - seg), (0, 0), (0, 0)))
        return t.reshape(b * dilation, nb, ATTN_BLOCK, h, dh)

    def with_prev(t):
        prev = jnp.pad(t[:, :-1], ((0, 0), (1, 0), (0, 0), (0, 0), (0, 0)))
        return jnp.concatenate([prev, t], axis=2)

    qb = to_blocks(q)
    kw = with_prev(to_blocks(k))
    vw = with_prev(to_blocks(v))

    qi = jnp.arange(ATTN_BLOCK)[:, None]
    kj = jnp.arange(2 * ATTN_BLOCK)[None, :]
    steps = qi + ATTN_BLOCK - kj
    blk = jnp.arange(nb)[:, None, None]
    valid = (steps >= 0) & (steps <= n_steps) & (blk * ATTN_BLOCK - ATTN_BLOCK + kj >= 0)

    logits = jnp.einsum('znqhd,znkhd->znhqk', qb, kw).astype(jnp.float32) * (HEAD_DIM ** -0.5) + bias
    logits = jnp.where(valid[None, :, None], logits, -jnp.inf)
    m = jnp.max(logits, axis=-1, keepdims=True)
    p = jnp.exp(logits - m)
    den = jnp.sum(p, axis=-1, keepdims=True)
    o = jnp.einsum('znhqk,znkhd->znqhd', p, vw.astype(jnp.float32)) / jnp.swapaxes(den, 2, 3)
    lse = jnp.swapaxes((m + jnp.log(den))[..., 0], 2, 3)

    o = o.reshape(b, dilation, segp, h, dh)[:, :, :seg]
    o = jnp.swapaxes(o, 1, 2).reshape(b, s, h, dh)
    lse = lse.reshape(b, dilation, segp, h)[:, :, :seg]
    lse = jnp.swapaxes(lse, 1, 2).reshape(b, s, h)
    return o, lse


def causal_depthwise_conv(x, w, bias):
    c = x.shape[-1]
    y = lax.conv_general_dilated(x, w[:, None, :], window_strides=(1,),
                                 padding=((CONV_WIDTH - 1, 0),),
                                 dimension_numbers=('NWC', 'WIO', 'NWC'),
                                 feature_group_count=c)
    return y + bias


def ssd_chunked_scan(xh, dt, a, bm, cm):
    b, s = xh.shape[:2]
    nc = s // SSD_CHUNK

    def chunks(t):
        return jnp.swapaxes(t.reshape(b, nc, SSD_CHUNK, *t.shape[2:]), 0, 1)

    causal = jnp.tril(jnp.ones((SSD_CHUNK, SSD_CHUNK), dtype=bool))[None, :, :, None, None]

    def step(state, inp):
        x_, dt_, b_, c_ = inp
        la = jnp.cumsum(dt_ * a, axis=1)
        seg = la[:, :, None] - la[:, None, :]
        decay = jnp.exp(jnp.where(causal, seg, -jnp.inf))
        cb = jnp.einsum('bign,bjgn->bijg', c_, b_)
        xdt = x_ * dt_[..., None]
        y_intra = jnp.einsum('bijg,bijgh,bjghp->bighp', cb, decay, xdt)
        y_inter = jnp.einsum('bign,bghpn->bighp', c_, state) * jnp.exp(la)[..., None]
        to_end = jnp.exp(la[:, -1:] - la)
        new_state = (state * jnp.exp(la[:, -1])[..., None, None]
                     + jnp.einsum('bjgn,bjgh,bjghp->bghpn', b_, to_end, xdt))
        return new_state, y_intra + y_inter

    state0 = jnp.zeros((b, N_SSM_GROUPS, HEADS_PER_SSM_GROUP, SSM_HEAD_DIM, D_STATE), jnp.float32)
    _, ys = lax.scan(step, state0, (chunks(xh), chunks(dt), chunks(bm), chunks(cm)))
    return jnp.swapaxes(ys, 0, 1).reshape(b, s, N_SSM_GROUPS, HEADS_PER_SSM_GROUP, SSM_HEAD_DIM)


def hybrid_mixer(xn, w_in, conv_w, conv_b, dt_bias, a_log, d_skip, ssm_norm_w,
                 w_attn_branch, w_ssm_branch, w_out, attn_biases):
    b, s, _ = xn.shape
    proj = xn @ w_in
    q, k, v, z, xbc, dt_raw, gate_logits = jnp.split(
        proj, np.cumsum([ATTN_WIDTH, ATTN_WIDTH, ATTN_WIDTH, D_INNER, XBC_WIDTH, N_SSM_HEADS]).tolist(), axis=-1)

    q = q.reshape(b, s, N_DIL, HEADS_PER_GROUP, HEAD_DIM)
    k = k.reshape(b, s, N_DIL, HEADS_PER_GROUP, HEAD_DIM)
    v = v.reshape(b, s, N_DIL, HEADS_PER_GROUP, HEAD_DIM)
    outs, lses = [], []
    for g, (window, dil) in enumerate(DILATED_GROUPS):
        o_g, lse_g = dilated_window_attention(q[:, :, g], k[:, :, g], v[:, :, g],
                                              attn_biases[g], dil, window // dil)
        outs.append(o_g)
        lses.append(lse_g)
    o = jnp.stack(outs, axis=2)
    alpha = jax.nn.softmax(jnp.stack(lses, axis=2), axis=2)
    attn = jnp.sum(alpha[..., None] * o, axis=2).reshape(b, s, ATTN_OUT_WIDTH).astype(xn.dtype)

    xbc = jax.nn.silu(causal_depthwise_conv(xbc, conv_w, conv_b))
    xs, bm, cm = jnp.split(xbc, [D_INNER, D_INNER + N_SSM_GROUPS * D_STATE], axis=-1)
    xh = xs.reshape(b, s, N_SSM_GROUPS, HEADS_PER_SSM_GROUP, SSM_HEAD_DIM).astype(jnp.float32)
    bm = bm.reshape(b, s, N_SSM_GROUPS, D_STATE).astype(jnp.float32)
    cm = cm.reshape(b, s, N_SSM_GROUPS, D_STATE).astype(jnp.float32)
    dt = jax.nn.softplus(dt_raw.astype(jnp.float32) + dt_bias.astype(jnp.float32))
    dt = dt.reshape(b, s, N_SSM_GROUPS, HEADS_PER_SSM_GROUP)
    a = -jnp.exp(a_log.astype(jnp.float32)).reshape(N_SSM_GROUPS, HEADS_PER_SSM_GROUP)
    y = ssd_chunked_scan(xh, dt, a, bm, cm)
    y = y + xh * d_skip.astype(jnp.float32).reshape(N_SSM_GROUPS, HEADS_PER_SSM_GROUP)[..., None]
    yg = (y.reshape(b, s, D_INNER) * jax.nn.silu(z.astype(jnp.float32))).reshape(b, s, N_SSM_GROUPS, D_INNER // N_SSM_GROUPS)
    yg = yg * lax.rsqrt(jnp.mean(yg * yg, axis=-1, keepdims=True) + EPS)
    ssm = (yg.reshape(b, s, D_INNER) * ssm_norm_w.astype(jnp.float32)).astype(xn.dtype)

    gates = jax.nn.sigmoid(gate_logits.astype(jnp.float32)).reshape(b, s, N_BRANCHES, D_MODEL)
    merged = (gates[:, :, 0] * (attn @ w_attn_branch).astype(jnp.float32)
              + gates[:, :, 1] * (ssm @ w_ssm_branch).astype(jnp.float32)).astype(xn.dtype)
    return merged @ w_out


def swiglu(xn, w_ffn_in, w_ffn_out):
    gate, up = jnp.split(xn @ w_ffn_in, 2, axis=-1)
    return (jax.nn.silu(gate) * up) @ w_ffn_out


def setup_inputs(seed: int = 0) -> dict:
    key = jax.random.key(seed)
    ks = jax.random.split(key, 20)
    f32 = jnp.float32
    nrm = lambda k, shape, scale: jax.random.normal(k, shape, f32) * scale
    dt0 = jnp.exp(jax.random.uniform(ks[5], (DEPTH, N_SSM_HEADS), f32, math.log(1e-3), math.log(1e-1)))
    return {
        "x": nrm(ks[0], (BATCH, SEQ, D_MODEL), 1.0),
        "norm1_w": 1.0 + nrm(ks[1], (DEPTH, D_MODEL), 0.02),
        "w_in": nrm(ks[2], (DEPTH, D_MODEL, IN_PROJ_WIDTH), D_MODEL ** -0.5),
        "conv_w": nrm(ks[3], (DEPTH, CONV_WIDTH, XBC_WIDTH), CONV_WIDTH ** -0.5),
        "conv_b": nrm(ks[4], (DEPTH, XBC_WIDTH), 0.02),
        "dt_bias": dt0 + jnp.log(-jnp.expm1(-dt0)),
        "a_log": jnp.log(jax.random.uniform(ks[6], (DEPTH, N_SSM_HEADS), f32, 1.0, 16.0)),
        "d_skip": 1.0 + nrm(ks[7], (DEPTH, N_SSM_HEADS), 0.1),
        "ssm_norm_w": 1.0 + nrm(ks[8], (DEPTH, D_INNER), 0.02),
        "w_attn_branch": nrm(ks[9], (DEPTH, ATTN_OUT_WIDTH, D_MODEL), ATTN_OUT_WIDTH ** -0.5),
        "w_ssm_branch": nrm(ks[10], (DEPTH, D_INNER, D_MODEL), D_INNER ** -0.5),
        "w_out": nrm(ks[11], (DEPTH, D_MODEL, D_MODEL), D_MODEL ** -0.5),
        "norm2_w": 1.0 + nrm(ks[12], (DEPTH, D_MODEL), 0.02),
        "w_ffn_in": nrm(ks[13], (DEPTH, D_MODEL, 2 * D_FF), D_MODEL ** -0.5),
        "w_ffn_out": nrm(ks[14], (DEPTH, D_FF, D_MODEL), D_FF ** -0.5),
        "rel_bias": nrm(ks[15], (N_REL_BUCKETS, N_ATTN_HEADS), 0.5),
        "final_norm_w": 1.0 + nrm(ks[16], (D_MODEL,), 0.02),
    }


def reference(x, norm1_w, w_in, conv_w, conv_b, dt_bias, a_log, d_skip, ssm_norm_w,
              w_attn_branch, w_ssm_branch, w_out, norm2_w, w_ffn_in, w_ffn_out,
              rel_bias, final_norm_w):
    attn_biases = [rel_bias_block(rel_bias[:, g * HEADS_PER_GROUP:(g + 1) * HEADS_PER_GROUP], dil, window // dil)
                   for g, (window, dil) in enumerate(DILATED_GROUPS)]
    h = x
    for layer in range(DEPTH):
        h = h + hybrid_mixer(rmsnorm(h, norm1_w[layer]), w_in[layer], conv_w[layer], conv_b[layer],
                             dt_bias[layer], a_log[layer], d_skip[layer], ssm_norm_w[layer],
                             w_attn_branch[layer], w_ssm_branch[layer], w_out[layer], attn_biases)
        h = h + swiglu(rmsnorm(h, norm2_w[layer]), w_ffn_in[layer], w_ffn_out[layer])
    return rmsnorm(h, final_norm_w)
```

```python
import math
import numpy as np
from contextlib import ExitStack
import concourse.bass as bass
import concourse.mybir as mybir
from concourse.bass_utils import run_bass_kernel_spmd

F32 = mybir.dt.float32
BF16 = mybir.dt.bfloat16
AF = mybir.ActivationFunctionType
ALU = mybir.AluOpType

D = 1024
KC = 8
TT = 512
ST = 2048
HALO = 2048
Q0, K0, V0, Z0, X0, DT0, G0, NPROJ = 0, 1536, 3072, 4608, 6656, 9728, 9760, 11808
DFF = 2816
DIL = (1, 4, 16)
EPS = 1e-6
NSEG = 8
NDSEM = 24
WSLOT = 4096
NWSLOT = 4


class Buf:
    __slots__ = ("name", "w", "r")

    def __init__(self, name):
        self.name = name
        self.w = None
        self.r = {}


class Til:
    def __init__(self, name, t, nparts=1, pstride=None):
        self.name = name
        self.t = t
        self.parts = [Buf(f"{name}.{i}") for i in range(nparts)]
        self.pstride = pstride

    def __getitem__(self, key):
        return self.t[key]

    @property
    def all(self):
        return self.parts

    def p(self, *idx):
        return [self.parts[i] for i in idx]

    def raw(self, p0, np_, off, dims):
        return bass.AP(self.t, p0 * self.pstride + off, [[self.pstride, np_]] + [list(d) for d in dims])


class Eng:
    def __init__(self, key, sem):
        self.key = key
        self.sem = sem
        self.cnt = 0
        self.items = []
        self.seen = {}


class KB:
    def __init__(self, nc, stack):
        self.nc = nc
        self.stack = stack
        self.eng = {}
        for k in ("pe", "act", "dve", "pool", "sp"):
            sem = stack.enter_context(nc.semaphore("sem_" + k))
            self.eng[k] = Eng(k, sem)
        self.dsem = [stack.enter_context(nc.semaphore(f"dsem{i}")) for i in range(NDSEM)]
        self.dval = [0] * NDSEM
        self.dtok = [None] * NDSEM
        self.drr = 0
        self.uid = 0

    def sb(self, name, shape, dtype, nparts=1, stack=None):
        self.uid += 1
        st = stack if stack is not None else self.stack
        t = st.enter_context(self.nc.sbuf_tensor(f"{name}_{self.uid}", list(shape), dtype))
        ps = 1
        for s in shape[1:]:
            ps *= s
        return Til(name, t, nparts, ps)

    def ps(self, name, shape, dtype, nparts=1):
        t = self.stack.enter_context(self.nc.psum_tensor(name, list(shape), dtype))
        return Til(name, t, nparts)

    def dram(self, name, shape, dtype, kind="Internal", nparts=1):
        t = self.nc.dram_tensor(name, list(shape), dtype, kind=kind)
        return Til(name, t.ap(), nparts)

    def _wait(self, e, deps):
        for tok in deps:
            sem, val, src = tok
            if src == e.key and e.key == "pe":
                continue
            sid = id(sem)
            if e.seen.get(sid, 0) >= val:
                continue
            e.seen[sid] = val
            e.items.append(("wait", sem, val))

    def _deps(self, reads, writes):
        deps = []
        for b in reads:
            if b.w is not None:
                deps.append(b.w)
        for b in writes:
            if b.w is not None:
                deps.append(b.w)
            deps.extend(b.r.values())
        return deps

    def _mark(self, tok, reads, writes):
        sid = id(tok[0])
        for b in reads:
            b.r[sid] = tok
        for b in writes:
            b.w = tok
            b.r = {}

    def op(self, ek, fn, reads=(), writes=(), inc=True):
        e = self.eng[ek]
        self._wait(e, self._deps(reads, writes))
        tok = (e.sem, e.cnt + 1, ek)
        if inc:
            e.cnt += 1
        e.items.append(("op", fn, inc))
        self._mark(tok, reads, writes)
        return tok

    def dma(self, qk, out, in_, reads=(), writes=(), **kw):
        e = self.eng[qk]
        i = self.drr
        self.drr = (i + 1) % NDSEM
        deps = self._deps(reads, writes)
        if self.dtok[i] is not None:
            deps.append(self.dtok[i])
        self._wait(e, deps)
        self.dval[i] += 16
        tok = (self.dsem[i], self.dval[i], "dma")
        self.dtok[i] = tok
        e.items.append(("dma", out, in_, self.dsem[i], kw))
        self._mark(tok, reads, writes)
        return tok

    def barrier(self):
        toks = [t for t in self.dtok if t is not None]
        for k in ("pe", "act", "dve", "pool"):
            o = self.eng[k]
            if o.cnt > 0:
                toks.append((o.sem, o.cnt, k + "_b"))
        for k in ("pe", "act", "dve", "pool", "sp"):
            self._wait(self.eng[k], toks)

    def finish(self):
        self.barrier()

    def emit(self):
        nc = self.nc
        kb = self

        def replay(engobj, e):
            for it in e.items:
                if it[0] == "wait":
                    engobj.wait_ge(it[1], it[2])
                elif it[0] == "op":
                    ins = it[1](engobj)
                    if it[2]:
                        ins.then_inc(e.sem, 1)
                else:
                    engobj.dma_start(out=it[1], in_=it[2], **it[4]).then_inc(it[3], 16)

        with nc.Block() as block:
            @block.tensor
            def _(pe):
                replay(pe, kb.eng["pe"])

            @block.scalar
            def _(a):
                replay(a, kb.eng["act"])

            @block.vector
            def _(v):
                replay(v, kb.eng["dve"])

            @block.gpsimd
            def _(g):
                replay(g, kb.eng["pool"])

            @block.sync
            def _(s):
                replay(s, kb.eng["sp"])

    def matmul(self, out, lhsT, rhs, start, stop, reads, writes, inc=True):
        return self.op("pe", lambda pe: pe.matmul(out, lhsT, rhs, start=start, stop=stop), reads, writes, inc)

    def transpose(self, out, in_, ident, reads, writes, inc=True):
        return self.op("pe", lambda pe: pe.transpose(out, in_, ident), reads, writes, inc)

    def act(self, out, in_, func, reads, writes, bias=None, scale=1.0, accum_out=None):
        kw = {}
        if bias is not None:
            kw["bias"] = bias
        if accum_out is not None:
            kw["accum_out"] = accum_out
        return self.op("act", lambda a: a.activation(out, in_, func, scale=scale, **kw), reads, writes)

    def tt(self, eng, out, in0, in1, op, reads, writes):
        return self.op(eng, lambda v: v.tensor_tensor(out, in0, in1, op), reads, writes)

    def ts(self, eng, out, in0, s1, op0, reads, writes, s2=None, op1=None):
        if op1 is None:
            return self.op(eng, lambda v: v.tensor_scalar(out, in0, s1, None, op0), reads, writes)
        return self.op(eng, lambda v: v.tensor_scalar(out, in0, s1, s2, op0, op1), reads, writes)

    def stt(self, eng, out, in0, scalar, in1, op0, op1, reads, writes):
        return self.op(eng, lambda v: v.scalar_tensor_tensor(out, in0, scalar, in1, op0, op1), reads, writes)

    def copy(self, eng, out, in_, reads, writes):
        if eng == "act":
            return self.op(eng, lambda a: a.copy(out, in_), reads, writes)
        return self.op(eng, lambda v: v.tensor_copy(out, in_), reads, writes)

    def memset(self, eng, ap, val, writes):
        return self.op(eng, lambda v: v.memset(ap, val), (), writes)


class Prog:
    def __init__(self, cfg):
        self.cfg = cfg
        self.n_own = cfg["n_own"]
        self.nall = HALO + self.n_own
        self.nc = bass.Bass("TRN2", target_bir_lowering=False)
        self.stack = ExitStack()
        self.kb = KB(self.nc, self.stack)
        self.bi = 0
        self.ev = 0

    def din(self, name, shape, dtype=F32):
        return Til(name, self.nc.dram_tensor(name, list(shape), dtype, kind="ExternalInput").ap())

    def dout(self, name, shape, dtype=F32):
        return Til(name, self.nc.dram_tensor(name, list(shape), dtype, kind="ExternalOutput").ap())

    def dscr(self, name, shape, dtype):
        kind = "ExternalOutput" if self.cfg.get("debug") else "Internal"
        return Til(name, self.nc.dram_tensor(name, list(shape), dtype, kind=kind).ap())

    def bank(self):
        b = self.banks[self.bi]
        self.bi = (self.bi + 1) % 8
        p = b.parts[0]
        assert p.w is None or len(p.r) > 0, f"PSUM bank {b.name} handed out while its last write is still unread"
        return b

    def evac_eng(self):
        self.ev ^= 1
        return "act" if self.ev else "dve"

    def bcast_load(self, dst, src_ap_row, n):
        src = bass.AP(src_ap_row.tensor, src_ap_row.offset, [[0, 128], [1, n]])
        self.kb.dma("sp", dst[:, 0:n], src, writes=dst.all)

    def cast_weight(self, dst, src, rows, cols):
        step = 128
        for r0 in range(0, rows, step):
            r1 = min(rows, r0 + step)
            self.kb.dma("pool", dst[r0:r1, :], src[r0:r1, :], reads=src.all, writes=dst.all)

    def wplan(self, plan):
        self.wreq = list(plan)
        self.wi = 0
        self.wissued = 0
        self.wviews = {}

    def _wissue(self, j):
        src, row0, kcn, col0, ncols = self.wreq[j]
        slot = self.wslots[self.wrr]
        self.wrr = (self.wrr + 1) % NWSLOT
        view = slot.t[:, 0:kcn * ncols].rearrange("p (k c) -> p k c", k=kcn)
        self.kb.dma("sp", view, src.t[row0:row0 + kcn * 128, col0:col0 + ncols].rearrange("(k p) c -> p k c", p=128),
                    reads=src.all, writes=slot.all)
        self.wviews[j] = (slot, view)

    def wget(self):
        j = self.wi
        self.wi += 1
        while self.wissued < min(len(self.wreq), j + 3):
            self._wissue(self.wissued)
            self.wissued += 1
        return self.wviews.pop(j)

    def build(self):
        cfg, kb, nc = self.cfg, self.kb, self.nc
        n_own, nall = self.n_own, self.nall
        layers = cfg["layers"]
        I = {}
        I["h_in"] = self.din("h_in", [nall, D])
        I["cmat"] = self.din("cmat", [128, 4 * 128])
        I["biasT"] = self.din("biasT", [128, 24 * 2 * 128])
        I["flag"] = self.din("flag", [1, 1])
        I["cmix"] = self.din("cmix", [1, NSEG * NSEG])
        I["cmask"] = self.din("cmask", [1, NSEG])
        I["final_w"] = self.din("final_w", [1, D])
        W32 = {}
        for l in layers:
            W32[l] = dict(
                w_in=self.din(f"w_in{l}", [D, NPROJ]), w_attn=self.din(f"w_attn{l}", [512, D]),
                w_ssm=self.din(f"w_ssm{l}", [2048, D]), w_out=self.din(f"w_out{l}", [D, D]),
                w_ffn_in=self.din(f"w_ffn_in{l}", [D, 2 * DFF]), w_ffn_out=self.din(f"w_ffn_out{l}", [DFF, D]),
                norm1=self.din(f"norm1_{l}", [1, D]), norm2=self.din(f"norm2_{l}", [1, D]),
                ssmw=self.din(f"ssmw_{l}", [1, 2048]), dtb=self.din(f"dtb_{l}", [1, 32]),
                alog=self.din(f"alog_{l}", [1, 32]), dsk=self.din(f"dsk_{l}", [1, 32]),
                convw=self.din(f"convw_{l}", [128, 24 * 4]), convb=self.din(f"convb_{l}", [128, 24]),
            )
        self.I, self.W32 = I, W32
        WB = {}
        for l in layers:
            WB[l] = dict(
                w_in=self.dscr_w(f"wb_in{l}", [D, NPROJ]), w_attn=self.dscr_w(f"wb_attn{l}", [512, D]),
                w_ssm=self.dscr_w(f"wb_ssm{l}", [2048, D]), w_out=self.dscr_w(f"wb_out{l}", [D, D]),
                w_ffn_in=self.dscr_w(f"wb_ffn_in{l}", [D, 2 * DFF]), w_ffn_out=self.dscr_w(f"wb_ffn_out{l}", [DFF, D]),
            )
        self.WB = WB
        S = {}
        S["kT"] = self.dscr("kT_scr", [1536, nall], BF16)
        S["v"] = self.dscr("v_scr", [nall, 24 * 65], BF16)
        S["qT"] = self.dscr("qT_scr", [1536, n_own], BF16)
        S["gT"] = self.dscr("gT_scr", [2048, n_own], BF16)
        S["sT"] = self.dscr("sT_scr", [1024, n_own], F32)
        S["attnT"] = self.dscr("attnT_scr", [512, n_own], BF16)
        self.S = S
        if cfg.get("debug"):
            self.dbg_acc = self.dout("dbg_acc", [65, 2 * ST])
        self.banks = [kb.ps(f"bank{i}", [128, 512], F32) for i in range(8)]
        P = {}
        P["ident"] = kb.sb("ident", [128, 128], BF16)
        P["cf"] = kb.sb("cf", [128, 4, 128], F32)
        P["causal"] = kb.sb("causal", [128, 128], BF16)
        P["ones_f"] = kb.sb("ones_f", [128, 128], F32)
        P["flag"] = kb.sb("flagb", [128, 1], F32)
        P["S"] = kb.sb("Sst", [128, 2048], F32)
        P["Sb"] = kb.sb("Sbst", [128, 2048], BF16)
        P["carry"] = kb.sb("carry", [128, 24, 3], F32)
        P["latacc"] = kb.sb("latacc", [128, 32], F32)
        self.wslots = [kb.sb(f"wslot{i}", [128, WSLOT], BF16) for i in range(NWSLOT)]
        self.wrr = 0
        self.P = P
        kb.dma("sp", P["cf"][:], I["cmat"][:, :].rearrange("p (a b) -> p a b", a=4), writes=P["cf"].all)
        kb.copy("dve", P["ident"][:], P["cf"][:, 0, :], P["cf"].all, P["ident"].all)
        kb.copy("dve", P["causal"][:], P["cf"][:, 3, :], P["cf"].all, P["causal"].all)
        kb.memset("pool", P["ones_f"][:], 1.0, P["ones_f"].all)
        self.bcast_load(P["flag"], I["flag"][0:1, 0:1], 1)
        for l in layers:
            for k in ("w_in", "w_attn", "w_ssm", "w_out", "w_ffn_in", "w_ffn_out"):
                src, dst = W32[l][k], WB[l][k]
                self.cast_weight(dst, src, src.t.shape[0], src.t.shape[1])
        outs = {}
        for ph in (cfg["phases"] if cfg.get("stop") != "init" else []):
            if ph[0] == "P1":
                l = ph[1]
                o_ss = self.dout(f"ssum_out{l}", [128, 2048])
                o_lat = self.dout(f"lat_out{l}", [1, 32])
                self.run_P1(l, I["h_in"], o_ss, o_lat)
            elif ph[0] == "P2":
                l, last = ph[1], ph[2]
                i_ss = self.din(f"ssum_all{l}", [NSEG * 128, 2048])
                i_lat = self.din(f"lat_all{l}", [1, NSEG * 32])
                o_h = self.dout(f"h_out{l}", [n_own, D])
                self.run_P2(l, I["h_in"], i_ss, i_lat, o_h, last)
        kb.finish()
        kb.emit()
        self.stack.close()
        return self.nc

    def dscr_w(self, name, shape):
        return Til(name, self.nc.dram_tensor(name, list(shape), BF16, kind="Internal").ap())

    def load_layer_consts(self, l, st):
        kb, W = self.kb, self.W32[l]
        C = {}
        C["norm1"] = kb.sb("norm1b", [128, D], F32, stack=st)
        C["ssmw"] = kb.sb("ssmwb", [128, 2048], F32, stack=st)
        C["dtb"] = kb.sb("dtbb", [128, 32], F32, stack=st)
        C["A"] = kb.sb("Ab", [128, 32], F32, stack=st)
        C["dsk"] = kb.sb("dskb", [128, 32], F32, stack=st)
        C["convw"] = kb.sb("convw", [128, 24, 4], F32, stack=st)
        C["convb"] = kb.sb("convb", [128, 24], F32, stack=st)
        self.bcast_load(C["norm1"], W["norm1"][0:1, :], D)
        self.bcast_load(C["ssmw"], W["ssmw"][0:1, :], 2048)
        self.bcast_load(C["dtb"], W["dtb"][0:1, :], 32)
        self.bcast_load(C["A"], W["alog"][0:1, :], 32)
        self.bcast_load(C["dsk"], W["dsk"][0:1, :], 32)
        kb.dma("sp", C["convw"][:], W["convw"][:, :].rearrange("p (a b) -> p a b", a=24), writes=C["convw"].all)
        kb.dma("sp", C["convb"][:], W["convb"][:, :], writes=C["convb"].all)
        kb.act(C["A"][:], C["A"][:], AF.Exp, C["A"].all, C["A"].all)
        kb.ts("dve", C["A"][:], C["A"][:], -1.0, ALU.mult, C["A"].all, C["A"].all)
        return C

    def norm_chunk(self, h_tm, h_bufs, wb, xn_tm, xnT, c, T):
        kb = self.kb
        junk, ss = T["junk"], T["ss"]
        kb.act(junk[:, 0:D], h_tm, AF.Square, h_bufs, junk.all + ss.all, accum_out=ss[:, 0:1])
        kb.ts("dve", ss[:, 1:2], ss[:, 0:1], 1.0 / D, ALU.mult, ss.all, ss.all, s2=EPS, op1=ALU.add)
        kb.act(ss[:, 2:3], ss[:, 1:2], AF.Sqrt, ss.all, ss.all)
        kb.op("dve", lambda v: v.reciprocal(ss[:, 3:4], ss[:, 2:3]), ss.all, ss.all)
        kb.stt("dve", xn_tm[:, :], h_tm, ss[:, 3:4], wb[:, 0:D], ALU.mult, ALU.mult, h_bufs + ss.all + wb.all, xn_tm.all)
        b = self.bank()
        bv = b.t[:, 0:512].bitcast(BF16)
        for kc in range(KC):
            kb.transpose(bv[:, kc * 128:(kc + 1) * 128], xn_tm[:, kc * 128:(kc + 1) * 128], self.P["ident"][:],
                         xn_tm.all + self.P["ident"].all, b.all, inc=(kc == KC - 1))
        kb.copy(self.evac_eng(), xnT[:, :, c * 128:(c + 1) * 128], bv.rearrange("p (k t) -> p k t", k=KC), b.all, xnT.all)

    def alloc_p1(self, st):
        kb = self.kb
        T = {}
        T["h"] = kb.sb("p1h", [128, D], F32, stack=st)
        T["xn_tm"] = kb.sb("p1xn", [128, D], BF16, stack=st)
        T["xnT"] = kb.sb("p1xnT", [128, KC, TT], BF16, stack=st)
        T["junk"] = kb.sb("p1junk", [128, D], F32, stack=st)
        T["ss"] = kb.sb("p1ss", [128, 8], F32, stack=st)
        T["fmst"] = [kb.sb(f"p1fmst{i}", [128, 4, TT], BF16, stack=st) for i in range(2)]
        T["vst"] = [kb.sb(f"p1vst{i}", [128, 8, 65], BF16, stack=st) for i in range(3)]
        T["vsth"] = [kb.sb(f"p1vsth{i}", [128, 8, 65], BF16, stack=st) for i in range(2)]
        T["zs"] = kb.sb("p1zs", [128, 4, 2048], BF16, stack=st)
        T["xpre"] = [kb.sb(f"p1xpre{i}", [128, TT + 3], F32, stack=st) for i in range(2)]
        T["cacc"] = [kb.sb(f"p1cacc{i}", [128, TT], F32, stack=st) for i in range(2)]
        T["xT"] = kb.sb("p1xT", [128, 16, TT], BF16, stack=st)
        T["BT"] = kb.sb("p1BT", [128, 4, TT], BF16, stack=st)
        T["CT"] = kb.sb("p1CT", [128, 4, TT], BF16, stack=st)
        T["dt"] = kb.sb("p1dt", [128, 4, 32], F32, stack=st)
        T["dtA"] = kb.sb("p1dtA", [128, 4, 32], F32, stack=st)
        T["dtmp"] = kb.sb("p1dtmp", [128, 32], F32, stack=st)
        T["ssmT"] = kb.sb("p1ssmT", [128, 16, TT], BF16, stack=st)
        T["sst"] = [kb.sb(f"p1sst{i}", [128, TT], F32, stack=st) for i in range(2)]
        T["x_tm"] = kb.sb("p1x_tm", [128, 2048], BF16, stack=st)
        T["xdt"] = [kb.sb(f"p1xdt{i}", [128, 512], BF16, stack=st) for i in range(2)]
        T["xdte"] = [kb.sb(f"p1xdte{i}", [128, 512], BF16, stack=st) for i in range(2)]
        T["xD"] = [kb.sb(f"p1xD{i}", [128, 512], BF16, stack=st) for i in range(2)]
        T["B_tm"] = kb.sb("p1B_tm", [128, 512], BF16, stack=st)
        T["E3"] = kb.sb("p1E3", [128, 96], F32, stack=st)
        T["CBm"] = kb.sb("p1CBm", [128, 4, 128], BF16, stack=st)
        T["lhsD"] = [kb.sb(f"p1lhsD{i}", [128, 128], F32, stack=st) for i in range(4)]
        T["LT"] = [kb.sb(f"p1LT{i}", [128, 4, 128], BF16, stack=st) for i in range(2)]
        T["MT"] = [kb.sb(f"p1MT{i}", [128, 8, 128], BF16, stack=st) for i in range(2)]
        T["yint"] = [kb.sb(f"p1yint{i}", [128, 512], F32, stack=st) for i in range(2)]
        T["y"] = [kb.sb(f"p1y{i}", [128, 512], F32, stack=st) for i in range(2)]
        T["yg"] = kb.sb("p1yg", [128, 2048], F32, stack=st)
        T["ssq"] = kb.sb("p1ssq", [128, 16], F32, stack=st)
        T["ssm_tm"] = kb.sb("p1ssm_tm", [128, 2048], BF16, stack=st)
        T["stmp"] = [kb.sb(f"p1stmp{i}", [128, 512], F32, stack=st) for i in range(2)]
        for t in T["vst"] + T["vsth"]:
            kb.memset("pool", t[:, :, 64:65], 1.0, t.all)
        self.rr = {}
        return T

    def rot(self, T, name):
        lst = T[name]
        i = self.rr.get(name, 0)
        self.rr[name] = (i + 1) % len(lst)
        return lst[i]

    def fm_proj(self, xnT, ncols_total, nt, consume):
        kb = self.kb
        nch = ncols_total // 128
        ch = 0
        while ch < nch:
            slot, wv = self.wget()
            ncs = wv.shape[2] // 128
            for cs in range(ncs):
                b = self.bank()
                for kc in range(KC):
                    kb.matmul(b.t[:, 0:nt], wv[:, kc, cs * 128:(cs + 1) * 128], xnT[:, kc, 0:nt] if nt == TT else xnT[:, kc, TT - nt:TT],
                              kc == 0, kc == KC - 1, slot.all + xnT.all, b.all, inc=(kc == KC - 1))
                consume(ch, b, nt)
                ch += 1

    def p1_plan(self, l, fl):
        wb = self.WB[l]
        plan = []
        if fl["kv"]:
            plan += [(wb["w_in"], 0, 8, K0 + i * 512, 512) for i in range(3)]
            plan += [(wb["w_in"], 0, 8, V0 + i * 512, 512) for i in range(3)]
        if fl["q"]:
            plan += [(wb["w_in"], 0, 8, Q0 + i * 512, 512) for i in range(3)]
        if fl["z"]:
            plan += [(wb["w_in"], 0, 8, Z0 + i * 512, 512) for i in range(4)]
        if fl["ssd"]:
            plan += [(wb["w_in"], 0, 8, DT0, 32)]
        if fl["ssd"] or fl["carry"]:
            plan += [(wb["w_in"], 0, 8, X0 + i * 512, 512) for i in range(6)]
        if fl["gates"]:
            plan += [(wb["w_in"], 0, 8, G0 + i * 512, 512) for i in range(4)]
            plan += [(wb["w_ssm"], 0, 16, i * 256, 256) for i in range(4)]
        return plan

    def pass1_tile(self, l, ti, fl, T, C, h_src, is_halo):
        kb, P, S = self.kb, self.P, self.S
        t0 = ti * TT
        o0 = t0 - HALO
        self.wplan(self.p1_plan(l, fl))
        xnT = T["xnT"]
        for c in range(4):
            kb.dma("sp", T["h"][:, :], h_src.t[t0 + c * 128:t0 + (c + 1) * 128, :], reads=h_src.all, writes=T["h"].all)
            self.norm_chunk(T["h"][:, :], T["h"].all, C["norm1"], T["xn_tm"], xnT, c, T)
        if fl["kv"]:
            def cons_k(ch, b, nt):
                stg = T["fmst"][(ch // 4) % 2]
                kb.copy(self.evac_eng(), stg[:, ch % 4, :], b.t[:, :], b.all, stg.all)
                if ch % 4 == 3:
                    r0 = (ch // 4) * 512
                    kb.dma("sp", S["kT"].t[r0:r0 + 512, t0:t0 + TT].rearrange("(c p) t -> p c t", p=128), stg[:, :, :],
                           reads=stg.all, writes=S["kT"].all)
            self.fm_proj(xnT, 1536, TT, cons_k)
            for cb in range(3):
                slot, wv = self.wget()
                for c in range(4):
                    b = self.bank()
                    for kc in range(KC):
                        kb.matmul(b.t[:, :], xnT[:, kc, c * 128:(c + 1) * 128], wv[:, kc, :], kc == 0, kc == KC - 1,
                                  slot.all + xnT.all, b.all, inc=(kc == KC - 1))
                    stg = self.rot(T, "vsth" if is_halo else "vst")
                    kb.copy(self.evac_eng(), stg[:, :, 0:64], b.t[:, :].rearrange("p (h e) -> p h e", h=8), b.all, stg.all)
                    if is_halo:
                        kb.ts("pool", stg[:, :, :], stg[:, :, :], P["flag"][:, 0:1], ALU.mult, stg.all + P["flag"].all, stg.all)
                    kb.dma("sp", S["v"].t[t0 + c * 128:t0 + (c + 1) * 128, cb * 520:(cb + 1) * 520],
                           stg[:, :, :].rearrange("p h e -> p (h e)"), reads=stg.all, writes=S["v"].all)
        if fl["q"]:
            def cons_q(ch, b, nt):
                stg = T["fmst"][(ch // 4) % 2]
                kb.copy(self.evac_eng(), stg[:, ch % 4, :], b.t[:, :], b.all, stg.all)
                if ch % 4 == 3:
                    r0 = (ch // 4) * 512
                    kb.dma("sp", S["qT"].t[r0:r0 + 512, o0:o0 + TT].rearrange("(c p) t -> p c t", p=128), stg[:, :, :],
                           reads=stg.all, writes=S["qT"].all)
            self.fm_proj(xnT, 1536, TT, cons_q)
        if fl["z"]:
            for cb in range(4):
                slot, wv = self.wget()
                for c in range(4):
                    b = self.bank()
                    for kc in range(KC):
                        kb.matmul(b.t[:, :], xnT[:, kc, c * 128:(c + 1) * 128], wv[:, kc, :], kc == 0, kc == KC - 1,
                                  slot.all + xnT.all, b.all, inc=(kc == KC - 1))
                    kb.act(T["zs"][:, c, cb * 512:(cb + 1) * 512], b.t[:, :], AF.Silu, b.all, T["zs"].all)
        if fl["ssd"]:
            slot, wv = self.wget()
            for c in range(4):
                b = self.bank()
                for kc in range(KC):
                    kb.matmul(b.t[:, 0:32], xnT[:, kc, c * 128:(c + 1) * 128], wv[:, kc, :], kc == 0, kc == KC - 1,
                              slot.all + xnT.all, b.all, inc=(kc == KC - 1))
                kb.tt("dve", T["dtmp"][:, :], b.t[:, 0:32], C["dtb"][:, :], ALU.add, b.all + C["dtb"].all, T["dtmp"].all)
                kb.act(T["dtmp"][:, :], T["dtmp"][:, :], AF.Exp, T["dtmp"].all, T["dtmp"].all)
                kb.act(T["dt"][:, c, :], T["dtmp"][:, :], AF.Ln, T["dtmp"].all, T["dt"].all, bias=1.0)
                kb.tt("dve", T["dtA"][:, c, :], T["dt"][:, c, :], C["A"][:, :], ALU.mult, T["dt"].all + C["A"].all, T["dtA"].all)
        if fl["carry"] and not fl["ssd"]:
            def cons_c(ch, b, nt):
                kb.copy(self.evac_eng(), P["carry"][:, ch, :], b.t[:, TT - 3:TT], b.all, P["carry"].all)
            self.fm_proj(xnT, 3072, TT, cons_c)
        if fl["ssd"]:
            def cons_x(ch, b, nt):
                xp = self.rot(T, "xpre")
                ca = self.rot(T, "cacc")
                kb.copy("act", xp[:, 3:TT + 3], b.t[:, :], b.all, xp.all)
                kb.copy("pool", xp[:, 0:3], P["carry"][:, ch, :], P["carry"].all, xp.all)
                kb.copy("pool", P["carry"][:, ch, :], xp[:, TT:TT + 3], xp.all, P["carry"].all)
                cw, cbias = C["convw"], C["convb"]
                kb.ts("dve", ca[:, :], xp[:, 0:TT], cw[:, ch, 0:1], ALU.mult, xp.all + cw.all + cbias.all, ca.all,
                      s2=cbias[:, ch:ch + 1], op1=ALU.add)
                for k in (1, 2, 3):
                    kb.stt("dve", ca[:, :], xp[:, k:k + TT], cw[:, ch, k:k + 1], ca[:, :], ALU.mult, ALU.add,
                           xp.all + cw.all + ca.all, ca.all)
                if ch < 16:
                    dst, dall = T["xT"][:, ch, :], T["xT"].all
                elif ch < 20:
                    dst, dall = T["BT"][:, ch - 16, :], T["BT"].all
                else:
                    dst, dall = T["CT"][:, ch - 20, :], T["CT"].all
                kb.act(dst, ca[:, :], AF.Silu, ca.all, dall)
            self.fm_proj(xnT, 3072, TT, cons_x)
            for c in range(4):
                self.ssd_chunk(l, c, T, C, fl)
        if fl["gates"]:
            def cons_g(ch, b, nt):
                stg = T["fmst"][(ch // 4) % 2]
                kb.act(stg[:, ch % 4, :], b.t[:, :], AF.Sigmoid, b.all, stg.all)
                if ch % 4 == 3:
                    r0 = (ch // 4) * 512
                    kb.dma("sp", S["gT"].t[r0:r0 + 512, o0:o0 + TT].rearrange("(c p) t -> p c t", p=128), stg[:, :, :],
                           reads=stg.all, writes=S["gT"].all)
            self.fm_proj(xnT, 2048, TT, cons_g)
            for sl in range(4):
                slot, wv = self.wget()
                for cs in range(2):
                    oc = sl * 2 + cs
                    b = self.bank()
                    for kc in range(16):
                        kb.matmul(b.t[:, :], wv[:, kc, cs * 128:(cs + 1) * 128], T["ssmT"][:, kc, :], kc == 0, kc == 15,
                                  slot.all + T["ssmT"].all, b.all, inc=(kc == 15))
                    stg = self.rot(T, "sst")
                    kb.copy(self.evac_eng(), stg[:, :], b.t[:, :], b.all, stg.all)
                    kb.dma("sp", S["sT"].t[oc * 128:(oc + 1) * 128, o0:o0 + TT], stg[:, :], reads=stg.all, writes=S["sT"].all)

    def ssd_chunk(self, l, c, T, C, fl):
        kb, P = self.kb, self.P
        full = fl["y"]
        cs = slice(c * 128, (c + 1) * 128)
        ident = P["ident"]
        cf = P["cf"]
        dtA = T["dtA"]
        for half in range(2):
            b = self.bank()
            bv = b.t[:, 0:512].bitcast(BF16)
            for j in range(8):
                ch = half * 8 + j
                kb.transpose(bv[:, j * 128:(j + 1) * 128], T["xT"][:, ch, cs], ident[:], T["xT"].all + ident.all, b.all, inc=(j == 7))
            kb.copy("act", T["x_tm"][:, half * 1024:(half + 1) * 1024], bv, b.all, T["x_tm"].all)
        b = self.bank()
        bv = b.t[:, 0:256].bitcast(BF16)
        for g in range(4):
            kb.transpose(bv[:, g * 128:(g + 1) * 128], T["BT"][:, g, cs], ident[:], T["BT"].all + ident.all, b.all, inc=(g == 3))
        kb.copy("dve", T["B_tm"][:, :], bv, b.all, T["B_tm"].all)
        b = self.bank()
        kb.matmul(b.t[:, 0:32], cf[:, 1, :], dtA[:, c, :], True, True, cf.all + dtA.all, b.all, inc=False)
        kb.matmul(b.t[:, 32:64], cf[:, 2, :], dtA[:, c, :], True, True, cf.all + dtA.all, b.all, inc=False)
        kb.matmul(b.t[:, 64:96], P["ones_f"][:, :], dtA[:, c, :], True, True, P["ones_f"].all + dtA.all, b.all)
        E3 = T["E3"]
        kb.act(E3[:, :], b.t[:, 0:96], AF.Exp, b.all, E3.all)
        kb.tt("dve", P["latacc"][:, :], P["latacc"][:, :], b.t[:, 64:96], ALU.add, b.all + P["latacc"].all, P["latacc"].all)
        if full:
            b = self.bank()
            for g in range(4):
                kb.matmul(b.t[:, g * 128:(g + 1) * 128], T["BT"][:, g, cs], T["CT"][:, g, cs], True, True,
                          T["BT"].all + T["CT"].all, b.all, inc=(g == 3))
            cau = bass.AP(P["causal"].t, 0, [[128, 128], [0, 4], [1, 128]])
            kb.tt("dve", T["CBm"][:, :, :], b.t[:, :].rearrange("p (g i) -> p g i", g=4), cau, ALU.mult,
                  b.all + P["causal"].all, T["CBm"].all)
        for g in range(4):
            gs = slice(g * 512, (g + 1) * 512)
            xdt = self.rot(T, "xdt")
            xdte = self.rot(T, "xdte")
            dtb = bass.AP(T["dt"].t, c * 32 + g * 8, [[T["dt"].pstride, 128], [1, 8], [0, 64]])
            teb = bass.AP(E3.t, 32 + g * 8, [[E3.pstride, 128], [1, 8], [0, 64]])
            kb.tt("pool", xdt[:, :].rearrange("p (h e) -> p h e", h=8), T["x_tm"][:, gs].rearrange("p (h e) -> p h e", h=8), dtb,
                  ALU.mult, T["x_tm"].all + T["dt"].all, xdt.all)
            kb.tt("pool", xdte[:, :].rearrange("p (h e) -> p h e", h=8), xdt[:, :].rearrange("p (h e) -> p h e", h=8), teb,
                  ALU.mult, xdt.all + E3.all, xdte.all)
            if full:
                xD = self.rot(T, "xD")
                dsb = bass.AP(C["dsk"].t, g * 8, [[C["dsk"].pstride, 128], [1, 8], [0, 64]])
                kb.tt("pool", xD[:, :].rearrange("p (h e) -> p h e", h=8), T["x_tm"][:, gs].rearrange("p (h e) -> p h e", h=8), dsb,
                      ALU.mult, T["x_tm"].all + C["dsk"].all, xD.all)
                MT = self.rot(T, "MT")
                for hh in range(2):
                    bD = self.bank()
                    for q in range(4):
                        h = g * 8 + hh * 4 + q
                        lhs = self.rot(T, "lhsD")
                        kb.ts("pool", lhs[:, :], cf[:, 2, :], dtA[:, c, h:h + 1], ALU.mult, cf.all + dtA.all, lhs.all)
                        kb.matmul(bD.t[:, q * 128:(q + 1) * 128], lhs[:, :], cf[:, 1, :], True, True, lhs.all + cf.all, bD.all)
                    LT = self.rot(T, "LT")
                    kb.act(LT[:, :, :], bD.t[:, :].rearrange("p (q i) -> p q i", q=4), AF.Exp, bD.all, LT.all)
                    cbb = bass.AP(T["CBm"].t, g * 128, [[T["CBm"].pstride, 128], [0, 4], [1, 128]])
                    kb.tt("dve", MT[:, hh * 4:(hh + 1) * 4, :], LT[:, :, :], cbb, ALU.mult, LT.all + T["CBm"].all, MT.all)
                b1 = self.bank()
                b2 = self.bank()
                for hq in range(8):
                    h = g * 8 + hq
                    es = slice(hq * 64, (hq + 1) * 64)
                    kb.matmul(b1.t[:, es], MT[:, hq, :], xdt[:, es], True, False, MT.all + xdt.all, b1.all, inc=False)
                    kb.matmul(b1.t[:, es], ident[:, :], xD[:, es], False, True, ident.all + xD.all, b1.all, inc=(hq == 7))
                for hq in range(8):
                    h = g * 8 + hq
                    es = slice(hq * 64, (hq + 1) * 64)
                    kb.matmul(b2.t[:, es], T["CT"][:, g, cs], P["Sb"][:, h * 64:(h + 1) * 64], True, True,
                              T["CT"].all + P["Sb"].all, b2.all, inc=(hq == 7))
                yint = self.rot(T, "yint")
                elb = bass.AP(E3.t, g * 8, [[E3.pstride, 128], [1, 8], [0, 64]])
                kb.tt("dve", yint[:, :].rearrange("p (h e) -> p h e", h=8), b2.t[:, :].rearrange("p (h e) -> p h e", h=8), elb,
                      ALU.mult, b2.all + E3.all, yint.all)
                y = self.rot(T, "y")
                kb.tt("dve", y[:, :], b1.t[:, :], yint[:, :], ALU.add, b1.all + yint.all, y.all)
                kb.tt("pool", T["yg"][:, gs], y[:, :], T["zs"][:, c, gs], ALU.mult, y.all + T["zs"].all, T["yg"].all)
            b3 = self.bank()
            for hq in range(8):
                h = g * 8 + hq
                es = slice(hq * 64, (hq + 1) * 64)
                kb.matmul(b3.t[:, es], T["B_tm"][:, g * 128:(g + 1) * 128], xdte[:, es], True, True,
                          T["B_tm"].all + xdte.all, b3.all, inc=(hq == 7))
            stmp = self.rot(T, "stmp")
            deb = bass.AP(E3.t, 64 + g * 8, [[E3.pstride, 128], [1, 8], [0, 64]])
            Sg = P["S"][:, gs]
            kb.tt("pool", stmp[:, :].rearrange("p (h e) -> p h e", h=8), Sg.rearrange("p (h e) -> p h e", h=8), deb, ALU.mult,
                  P["S"].all + E3.all, stmp.all)
            kb.tt("dve", Sg, stmp[:, :], b3.t[:, :], ALU.add, stmp.all + b3.all, P["S"].all)
            kb.copy("pool", P["Sb"][:, gs], Sg, P["S"].all, P["Sb"].all)
        if full:
            ssq = T["ssq"]
            for g in range(4):
                gs = slice(g * 512, (g + 1) * 512)
                kb.act(T["junk"][:, 0:512], T["yg"][:, gs], AF.Square, T["yg"].all, T["junk"].all + ssq.all, accum_out=ssq[:, g:g + 1])
            kb.ts("dve", ssq[:, 4:8], ssq[:, 0:4], 1.0 / 512, ALU.mult, ssq.all, ssq.all, s2=EPS, op1=ALU.add)
            kb.act(ssq[:, 8:12], ssq[:, 4:8], AF.Sqrt, ssq.all, ssq.all)
            kb.op("dve", lambda v: v.reciprocal(ssq[:, 12:16], ssq[:, 8:12]), ssq.all, ssq.all)
            for g in range(4):
                gs = slice(g * 512, (g + 1) * 512)
                kb.stt("dve", T["ssm_tm"][:, gs], T["yg"][:, gs], ssq[:, 12 + g:13 + g], C["ssmw"][:, gs],
                       ALU.mult, ALU.mult, T["yg"].all + ssq.all + C["ssmw"].all, T["ssm_tm"].all)
            for half in range(2):
                b = self.bank()
                bv = b.t[:, 0:512].bitcast(BF16)
                for j in range(8):
                    ch = half * 8 + j
                    kb.transpose(bv[:, j * 128:(j + 1) * 128], T["ssm_tm"][:, ch * 128:(ch + 1) * 128], ident[:],
                                 T["ssm_tm"].all + ident.all, b.all, inc=(j == 7))
                kb.copy(self.evac_eng(), T["ssmT"][:, half * 8:(half + 1) * 8, cs], bv.rearrange("p (k t) -> p k t", k=8), b.all, T["ssmT"].all)

    def run_P1(self, l, h_src, o_ss, o_lat):
        kb, P = self.kb, self.P
        with ExitStack() as st:
            C = self.load_layer_consts(l, st)
            T = self.alloc_p1(st)
            kb.memset("pool", P["S"][:, :], 0.0, P["S"].all)
            kb.memset("pool", P["Sb"][:, :], 0.0, P["Sb"].all)
            kb.memset("pool", P["latacc"][:, :], 0.0, P["latacc"].all)
            fl_c = dict(kv=False, q=False, z=False, ssd=False, carry=True, gates=False, y=False)
            self.pass1_tile(l, 3, fl_c, T, C, h_src, True)
            fl_s = dict(kv=False, q=False, z=False, ssd=True, carry=False, gates=False, y=False)
            for ti in range(4, 4 + self.n_own // TT):
                self.pass1_tile(l, ti, fl_s, T, C, h_src, False)
            kb.dma("sp", o_ss.t[:, :], P["S"][:, :], reads=P["S"].all, writes=o_ss.all)
            kb.dma("sp", o_lat.t[0:1, :], P["latacc"][0:1, :], reads=P["latacc"].all, writes=o_lat.all)
            kb.barrier()

    def run_P2(self, l, h_src, i_ss, i_lat, o_h, last):
        kb, P = self.kb, self.P
        nst = self.n_own // ST
        with ExitStack() as st:
            LT_ = kb.sb("c_lat", [128, NSEG * 32], F32, stack=st)
            cm = kb.sb("c_mix", [128, NSEG * NSEG], F32, stack=st)
            cmask = kb.sb("c_mask", [128, NSEG], F32, stack=st)
            coef = kb.sb("c_coef", [128, 32], F32, stack=st)
            sl = kb.sb("c_sl", [128, 2048], F32, stack=st)
            self.bcast_load(LT_, i_lat[0:1, :], NSEG * 32)
            self.bcast_load(cm, self.I["cmix"][0:1, :], NSEG * NSEG)
            self.bcast_load(cmask, self.I["cmask"][0:1, :], NSEG)
            kb.memset("pool", P["S"][:, :], 0.0, P["S"].all)
            for j in range(NSEG):
                kb.ts("dve", coef[:, :], LT_[:, 0:32], cm[:, j * NSEG:j * NSEG + 1], ALU.mult, LT_.all + cm.all, coef.all)
                for k in range(1, NSEG):
                    kb.stt("dve", coef[:, :], LT_[:, k * 32:(k + 1) * 32], cm[:, j * NSEG + k:j * NSEG + k + 1], coef[:, :],
                           ALU.mult, ALU.add, LT_.all + cm.all + coef.all, coef.all)
                kb.act(coef[:, :], coef[:, :], AF.Exp, coef.all, coef.all)
                kb.ts("dve", coef[:, :], coef[:, :], cmask[:, j:j + 1], ALU.mult, coef.all + cmask.all, coef.all)
                kb.dma("sp", sl[:, :], i_ss.t[j * 128:(j + 1) * 128, :], reads=i_ss.all, writes=sl.all)
                cb_ = bass.AP(coef.t, 0, [[coef.pstride, 128], [1, 32], [0, 64]])
                kb.tt("dve", sl[:, :].rearrange("p (h e) -> p h e", h=32), sl[:, :].rearrange("p (h e) -> p h e", h=32), cb_, ALU.mult,
                      sl.all + coef.all, sl.all)
                kb.tt("dve", P["S"][:, :], P["S"][:, :], sl[:, :], ALU.add, P["S"].all + sl.all, P["S"].all)
            kb.copy("pool", P["Sb"][:, :], P["S"][:, :], P["S"].all, P["Sb"].all)
            kb.barrier()
        stop = self.cfg.get("stop")
        if stop == "sin":
            return
        with ExitStack() as st:
            C = self.load_layer_consts(l, st)
            T = self.alloc_p1(st)
            for ti in range(4):
                fl = dict(kv=True, q=False, z=False, ssd=False, carry=(ti == 3), gates=False, y=False)
                if stop == "halo_norm":
                    fl["kv"] = False
                    fl["carry"] = False
                if stop == "halo_kv":
                    fl["carry"] = False
                self.pass1_tile(l, ti, fl, T, C, h_src, True)
            kb.barrier()
        if stop in ("halo", "halo_norm", "halo_kv"):
            return
        for s in range(nst):
            with ExitStack() as st:
                C = self.load_layer_consts(l, st)
                T = self.alloc_p1(st)
                fl = dict(kv=True, q=True, z=True, ssd=True, carry=False, gates=True, y=True)
                for ti in range(4 + 4 * s, 8 + 4 * s):
                    self.pass1_tile(l, ti, fl, T, C, h_src, False)
                kb.barrier()
            if stop == "p1":
                return
            with ExitStack() as st:
                self.pass2_st(l, s, st)
                kb.barrier()
            if stop == "p2":
                return
            with ExitStack() as st:
                self.pass3_st(l, s, st, h_src, o_h, last)
                kb.barrier()

    def pass2_st(self, l, s, st):
        kb, P, S = self.kb, self.P, self.S
        expb = kb.sb("p2expb", [128, 24 * 2 * 128], BF16, stack=st)
        bstage = kb.sb("p2bst", [128, 2048], F32, stack=st)
        for i in range(3):
            kb.dma("sp", bstage[:, :], self.I["biasT"][:, i * 2048:(i + 1) * 2048], reads=self.I["biasT"].all, writes=bstage.all)
            kb.act(expb[:, i * 2048:(i + 1) * 2048], bstage[:, :], AF.Exp, bstage.all, expb.all)
        Qt = [kb.sb(f"p2Q{i}", [128, ST], BF16, stack=st) for i in range(2)]
        Kt = [kb.sb(f"p2K{i}", [128, 2 * ST], BF16, stack=st) for i in range(2)]
        Vt = [kb.sb(f"p2V{i}", [128, 32, 130], BF16, stack=st) for i in range(2)]
        acc = kb.sb("p2acc", [128, 2, ST], F32, stack=st)
        Pt = [kb.sb(f"p2P{i}", [128, 512], BF16, stack=st) for i in range(3)]
        rden = [kb.sb(f"p2rden{i}", [128, 512], F32, stack=st) for i in range(2)]
        ast = [kb.sb(f"p2ast{i}", [128, ST], BF16, stack=st) for i in range(2)]
        self.p2o = 0
        self.p2s = 0
        o0 = s * ST
        a0 = HALO + o0
        cnt = 0
        for hp in range(4):
            kb.memset("dve", acc[0:65, :, :], 0.0, acc.all)
            for g in range(3):
                d = DIL[g]
                nprev = 128 * d
                nbp = d
                row0 = g * 512 + hp * 128
                Q, Kx, V = Qt[cnt % 2], Kt[cnt % 2], Vt[cnt % 2]
                cnt += 1
                kb.dma("sp", Q[:, :], S["qT"].t[row0:row0 + 128, o0:o0 + ST], reads=S["qT"].all, writes=Q.all)
                kb.dma("sp", Kx[:, 0:nprev + ST], S["kT"].t[row0:row0 + 128, a0 - nprev:a0 + ST], reads=S["kT"].all, writes=Kx.all)
                vcol = (g * 8 + hp * 2) * 65
                vt = S["v"].t
                rowlen = 24 * 65

                def vsrc(tok0, nblk_r):
                    return bass.AP(vt.tensor, vt.offset + tok0 * rowlen + vcol, [[d * rowlen, 128], [rowlen, nblk_r], [1, 130]])
                G = min(4, d)
                for r0 in range(0, d, G):
                    kb.dma("sp", V[:, r0:r0 + G, :], vsrc(a0 - nprev + r0, G), reads=S["v"].all, writes=V.all)
                nbc = 16 // d
                for bb in range(nbc):
                    for r0 in range(0, d, G):
                        kb.dma("sp", V[:, nbp + bb * d + r0:nbp + bb * d + r0 + G, :], vsrc(a0 + bb * 128 * d + r0, G),
                               reads=S["v"].all, writes=V.all)
                nblk = 16
                cut = self.cfg.get("p2cut", 9)
                for q4 in range(4 if cut >= 1.2 else 0):
                    self.p2o ^= 1
                    bo = [self.banks[self.p2o * 2], self.banks[self.p2o * 2 + 1]]
                    for qq in range(4):
                        bi = q4 * 4 + qq
                        bb, r = bi // d, bi % d

                        def ksl(base_part, col0):
                            return bass.AP(Kx.t, base_part * Kx.pstride + col0, [[Kx.pstride, 64], [d, 128]])
                        kcur0 = nprev + bb * 128 * d + r
                        kprev0 = kcur0 - 128 * d
                        vcur = nbp + bi
                        vprev = vcur - d
                        self.p2s ^= 1
                        bSs = [self.banks[4 + self.p2s * 2], self.banks[5 + self.p2s * 2]]
                        for hh in range(2):
                            bS = bSs[hh]
                            qh = bass.AP(Q.t, hh * 64 * Q.pstride + bb * 128 * d + r, [[Q.pstride, 64], [d, 128]])
                            kb.matmul(bS.t[:, 0:128], ksl(hh * 64, kprev0), qh, True, True, Kx.all + Q.all, bS.all, inc=False)
                            kb.matmul(bS.t[:, 128:256], ksl(hh * 64, kcur0), qh, True, True, Kx.all + Q.all, bS.all, inc=True)
                        Pm = Pt[(bi + cnt) % 3]
                        if cut >= 1.5:
                            for hh in range(2):
                                kb.act(Pm[:, hh * 256:(hh + 1) * 256], bSs[hh].t[:, 0:256], AF.Exp, bSs[hh].all, Pm.all, scale=0.125)
                        eb = expb[:, (g * 8 + hp * 2) * 256:(g * 8 + hp * 2 + 2) * 256]
                        if cut >= 2:
                            kb.tt("dve", Pm[:, :], Pm[:, :], eb, ALU.mult, Pm.all + expb.all, Pm.all)
                        for hh in range(2 if cut >= 3 else 0):
                            kb.matmul(bo[hh].t[0:65, qq * 128:(qq + 1) * 128], V[:, vprev, hh * 65:(hh + 1) * 65], Pm[:, hh * 256:hh * 256 + 128],
                                      True, False, V.all + Pm.all, bo[hh].all, inc=False)
                            kb.matmul(bo[hh].t[0:65, qq * 128:(qq + 1) * 128], V[:, vcur, hh * 65:(hh + 1) * 65], Pm[:, hh * 256 + 128:hh * 256 + 256],
                                      False, True, V.all + Pm.all, bo[hh].all, inc=True)
                    for hh in range(2 if cut >= 4 else 0):
                        if d == 1:
                            oap = bass.AP(acc.t, hh * ST + q4 * 512, [[acc.pstride, 65], [128, 4], [1, 128]])
                        elif d == 4:
                            oap = bass.AP(acc.t, hh * ST + q4 * 512, [[acc.pstride, 65], [1, 4], [4, 128]])
                        else:
                            oap = bass.AP(acc.t, hh * ST + q4 * 4, [[acc.pstride, 65], [1, 4], [16, 128]])
                        kb.tt("dve", oap, oap, bo[hh].t[0:65, :].rearrange("p (a b) -> p a b", a=4), ALU.add, acc.all + bo[hh].all, acc.all)
            if self.cfg.get("debug") and hp == 3 and s == 0:
                kb.dma("sp", self.dbg_acc.t[:, :], acc[0:65, :, :].rearrange("p a b -> p (a b)"), reads=acc.all, writes=self.dbg_acc.all)
            for hh in range(2 if self.cfg.get("p2cut", 9) >= 5 else 0):
                A_ = ast[hh]
                for t4 in range(4):
                    b = self.banks[4 + (t4 % 4)]
                    kb.matmul(b.t[0:64, :], P["ones_f"][64:65, 0:64], acc[64:65, hh, t4 * 512:(t4 + 1) * 512], True, True,
                              P["ones_f"].all + acc.all, b.all)
                    rd = rden[t4 % 2]
                    kb.op("dve", lambda v, rd=rd, b=b: v.reciprocal(rd[0:64, :], b.t[0:64, :]), b.all, rd.all)
                    kb.tt("dve", A_[0:64, t4 * 512:(t4 + 1) * 512], acc[0:64, hh, t4 * 512:(t4 + 1) * 512], rd[0:64, :], ALU.mult,
                          acc.all + rd.all, A_.all)
                hd = hp * 2 + hh
                kb.dma("sp", S["attnT"].t[hd * 64:(hd + 1) * 64, o0:o0 + ST], A_[0:64, :], reads=A_.all, writes=S["attnT"].all)

    def pass3_st(self, l, s, st, h_src, o_h, last):
        kb, P, S = self.kb, self.P, self.S
        W, wb = self.W32[l], self.WB[l]
        norm2 = kb.sb("p3n2", [128, D], F32, stack=st)
        self.bcast_load(norm2, W["norm2"][0:1, :], D)
        if last:
            fin = kb.sb("p3fin", [128, D], F32, stack=st)
            self.bcast_load(fin, self.I["final_w"][0:1, :], D)
        T = {}
        T["junk"] = kb.sb("p3junk", [128, D], F32, stack=st)
        T["ss"] = kb.sb("p3ss", [128, 8], F32, stack=st)
        ht = kb.sb("p3h", [128, 4, D], F32, stack=st, nparts=4)
        aT = kb.sb("p3aT", [128, 4, TT], BF16, stack=st)
        sTt = [kb.sb(f"p3sT{i}", [128, TT], F32, stack=st) for i in range(2)]
        g0t = [kb.sb(f"p3g0{i}", [128, TT], BF16, stack=st) for i in range(2)]
        g1t = [kb.sb(f"p3g1{i}", [128, TT], BF16, stack=st) for i in range(2)]
        mtmp = [kb.sb(f"p3mt{i}", [128, TT], F32, stack=st) for i in range(2)]
        mT = kb.sb("p3mT", [128, KC, TT], BF16, stack=st)
        xn_tm = kb.sb("p3xn", [128, D], BF16, stack=st)
        xnT = kb.sb("p3xnT", [128, KC, TT], BF16, stack=st)
        hid = kb.sb("p3hid", [128, 22, TT], BF16, stack=st)
        gst = [kb.sb(f"p3gst{i}", [128, TT], BF16, stack=st) for i in range(2)]
        ost = [kb.sb(f"p3ost{i}", [128, D], F32, stack=st) for i in range(2)]
        for tl in range(4):
            o0 = s * ST + tl * TT
            a0 = HALO + o0
            plan = [(wb["w_attn"], 0, 4, 0, 1024)]
            plan += [(wb["w_out"], 0, 8, i * 512, 512) for i in range(2)]
            plan += [(wb["w_ffn_in"], 0, 8, DFF * hf + ch4 * 512, 512 if ch4 < 5 else 256) for ch4 in range(6) for hf in range(2)]
            plan += [(wb["w_ffn_out"], k0 * 128, kn, cb * 512, 512) for cb in range(2) for (k0, kn) in ((0, 8), (8, 8), (16, 6))]
            self.wplan(plan)
            kb.dma("sp", ht[:, :, :], h_src.t[a0:a0 + TT, :].rearrange("(c p) d -> p c d", p=128), reads=h_src.all, writes=ht.all)
            kb.dma("sp", aT[:, :, :], S["attnT"].t[:, o0:o0 + TT].rearrange("(c p) t -> p c t", p=128), reads=S["attnT"].all, writes=aT.all)
            slot, wv = self.wget()
            for oc in range(8):
                i2 = oc % 2
                kb.dma("sp", sTt[i2][:, :], S["sT"].t[oc * 128:(oc + 1) * 128, o0:o0 + TT], reads=S["sT"].all, writes=sTt[i2].all)
                kb.dma("sp", g0t[i2][:, :], S["gT"].t[oc * 128:(oc + 1) * 128, o0:o0 + TT], reads=S["gT"].all, writes=g0t[i2].all)
                kb.dma("sp", g1t[i2][:, :], S["gT"].t[1024 + oc * 128:1024 + (oc + 1) * 128, o0:o0 + TT], reads=S["gT"].all, writes=g1t[i2].all)
                b = self.bank()
                for kc in range(4):
                    kb.matmul(b.t[:, :], wv[:, kc, oc * 128:(oc + 1) * 128], aT[:, kc, :], kc == 0, kc == 3, slot.all + aT.all, b.all, inc=(kc == 3))
                kb.tt("dve", mtmp[i2][:, :], b.t[:, :], g0t[i2][:, :], ALU.mult, b.all + g0t[i2].all, mtmp[i2].all)
                kb.tt("pool", sTt[i2][:, :], sTt[i2][:, :], g1t[i2][:, :], ALU.mult, sTt[i2].all + g1t[i2].all, sTt[i2].all)
                kb.tt("dve", mT[:, oc, :], mtmp[i2][:, :], sTt[i2][:, :], ALU.add, mtmp[i2].all + sTt[i2].all, mT.all)
            for cb in range(2):
                slot, wv = self.wget()
                for c in range(4):
                    b = self.bank()
                    for kc in range(KC):
                        kb.matmul(b.t[:, :], mT[:, kc, c * 128:(c + 1) * 128], wv[:, kc, :], kc == 0, kc == KC - 1, slot.all + mT.all, b.all,
                                  inc=(kc == KC - 1))
                    kb.tt("dve", ht[:, c, cb * 512:(cb + 1) * 512], ht[:, c, cb * 512:(cb + 1) * 512], b.t[:, :], ALU.add,
                          ht.p(c) + b.all, ht.p(c))
            for c in range(4):
                self.norm_chunk(ht[:, c, :], ht.p(c), norm2, xn_tm, xnT, c, T)
            for ch4 in range(6):
                slot_g, wg = self.wget()
                slot_u, wu = self.wget()
                ncs = wg.shape[2] // 128
                for cs_ in range(ncs):
                    ch = ch4 * 4 + cs_
                    bg = self.bank()
                    for kc in range(KC):
                        kb.matmul(bg.t[:, :], wg[:, kc, cs_ * 128:(cs_ + 1) * 128], xnT[:, kc, :], kc == 0, kc == KC - 1,
                                  slot_g.all + xnT.all, bg.all, inc=(kc == KC - 1))
                    bu = self.bank()
                    for kc in range(KC):
                        kb.matmul(bu.t[:, :], wu[:, kc, cs_ * 128:(cs_ + 1) * 128], xnT[:, kc, :], kc == 0, kc == KC - 1,
                                  slot_u.all + xnT.all, bu.all, inc=(kc == KC - 1))
                    gs_ = gst[ch % 2]
                    kb.act(gs_[:, :], bg.t[:, :], AF.Silu, bg.all, gs_.all)
                    kb.tt("dve", hid[:, ch, :], bu.t[:, :], gs_[:, :], ALU.mult, bu.all + gs_.all, hid.all)
            for cb in range(2):
                bks = [self.bank() for _ in range(4)]
                for (k0, kn) in ((0, 8), (8, 8), (16, 6)):
                    slot, wv = self.wget()
                    for c in range(4):
                        for kk in range(kn):
                            kc = k0 + kk
                            kb.matmul(bks[c].t[:, :], hid[:, kc, c * 128:(c + 1) * 128], wv[:, kk, :], kc == 0, kc == 21,
                                      slot.all + hid.all, bks[c].all, inc=(kk == kn - 1))
                for c in range(4):
                    kb.tt("dve", ht[:, c, cb * 512:(cb + 1) * 512], ht[:, c, cb * 512:(cb + 1) * 512], bks[c].t[:, :], ALU.add,
                          ht.p(c) + bks[c].all, ht.p(c))
            for c in range(4):
                if last:
                    o = ost[c % 2]
                    junk, ss = T["junk"], T["ss"]
                    kb.act(junk[:, 0:D], ht[:, c, :], AF.Square, ht.p(c), junk.all + ss.all, accum_out=ss[:, 0:1])
                    kb.ts("dve", ss[:, 1:2], ss[:, 0:1], 1.0 / D, ALU.mult, ss.all, ss.all, s2=EPS, op1=ALU.add)
                    kb.act(ss[:, 2:3], ss[:, 1:2], AF.Sqrt, ss.all, ss.all)
                    kb.op("dve", lambda v, ss=ss: v.reciprocal(ss[:, 3:4], ss[:, 2:3]), ss.all, ss.all)
                    kb.stt("dve", o[:, :], ht[:, c, :], ss[:, 3:4], fin[:, 0:D], ALU.mult, ALU.mult, ht.p(c) + ss.all + fin.all, o.all)
                    kb.dma("sp", o_h.t[o0 + c * 128:o0 + (c + 1) * 128, :], o[:, :], reads=o.all, writes=o_h.all)
                else:
                    kb.dma("sp", o_h.t[o0 + c * 128:o0 + (c + 1) * 128, :], ht[:, c, :], reads=ht.p(c), writes=o_h.all)


def _t5_bucket(dist):
    max_exact = 16
    d_f = np.maximum(dist, 1).astype(np.float32)
    large = max_exact + (np.log(d_f / np.float32(max_exact)) / np.float32(math.log(2048 / max_exact)) * np.float32(32 - max_exact)).astype(np.int32)
    large = np.minimum(large, 31)
    return np.where(dist < max_exact, dist, large)


def make_consts(rel_bias):
    k = np.arange(128)[:, None]
    q = np.arange(128)[None, :]
    cmat = np.zeros((128, 4, 128), np.float32)
    cmat[:, 0, :] = np.eye(128)
    cmat[:, 1, :] = (k <= q)
    cmat[:, 2, :] = (k > q)
    cmat[:, 3, :] = (q >= k)
    biasT = np.full((128, 24, 2, 128), -30000.0, np.float32)
    for g, d in enumerate(DIL):
        for kbi, (steps, valid) in enumerate(((q - k + 128, q <= k), (q - k, k <= q))):
            bucket = _t5_bucket(np.clip(steps, 0, 128) * d)
            for hh in range(8):
                h = g * 8 + hh
                vals = rel_bias[bucket, h]
                biasT[:, h, kbi, :] = np.where(valid, vals, np.float32(-30000.0))
    return cmat.reshape(128, 512), biasT.reshape(128, 24 * 2 * 128)


def percore_consts(core, cores_per_seq):
    s = core % cores_per_seq
    base = core - s
    cmix = np.zeros((NSEG, NSEG), np.float32)
    cmask = np.zeros((NSEG,), np.float32)
    for j in range(s):
        cmask[base + j] = 1.0
        for k in range(j + 1, s):
            cmix[base + j, base + k] = 1.0
    flag = np.array([[1.0 if s > 0 else 0.0]], np.float32)
    return flag, cmix.reshape(1, -1), cmask.reshape(1, -1)


_PROG_CACHE = {}


def layer_inputs(l, p):
    cw = np.ascontiguousarray(p["conv_w"][l].T.reshape(24, 128, 4).transpose(1, 0, 2)).reshape(128, 96)
    cb = np.ascontiguousarray(p["conv_b"][l].reshape(24, 128).T)
    return {
        f"w_in{l}": p["w_in"][l], f"w_attn{l}": p["w_attn_branch"][l], f"w_ssm{l}": p["w_ssm_branch"][l],
        f"w_out{l}": p["w_out"][l], f"w_ffn_in{l}": p["w_ffn_in"][l], f"w_ffn_out{l}": p["w_ffn_out"][l],
        f"norm1_{l}": p["norm1_w"][l][None, :], f"norm2_{l}": p["norm2_w"][l][None, :],
        f"ssmw_{l}": p["ssm_norm_w"][l][None, :], f"dtb_{l}": p["dt_bias"][l][None, :],
        f"alog_{l}": p["a_log"][l][None, :], f"dsk_{l}": p["d_skip"][l][None, :],
        f"convw_{l}": cw, f"convb_{l}": cb,
    }


def run_layers(x, p, n_cores, cores_per_seq, n_own, depth, debug=False):
    B, Sq, _ = x.shape
    cmat, biasT = make_consts(p["rel_bias"])
    h = x
    dbg = None
    for l in range(depth):
        h_ins = []
        for c in range(n_cores):
            b, s = c // cores_per_seq, c % cores_per_seq
            own = h[b, s * n_own:(s + 1) * n_own]
            if s == 0:
                halo = np.zeros((HALO, D), np.float32)
            else:
                halo = h[b, s * n_own - HALO:s * n_own]
            h_ins.append(np.ascontiguousarray(np.concatenate([halo, own], 0)))
        common = {"cmat": cmat, "biasT": biasT, "final_w": p["final_norm_w"][None, :]}
        common.update(layer_inputs(l, p))
        pcs = [percore_consts(c, cores_per_seq) for c in range(n_cores)]
        key = ("P1", n_own, l)
        prog = Prog(dict(n_own=n_own, layers=[l], phases=[("P1", l)], debug=False))
        nc1 = prog.build()
        maps = []
        for c in range(n_cores):
            m = dict(common)
            m.update({"h_in": h_ins[c], "flag": pcs[c][0], "cmix": pcs[c][1], "cmask": pcs[c][2]})
            maps.append(m)
        r1 = run_bass_kernel_spmd(nc1, maps, core_ids=list(range(n_cores)))
        ss_all = np.zeros((NSEG * 128, 2048), np.float32)
        lat_all = np.zeros((1, NSEG * 32), np.float32)
        for c in range(n_cores):
            ss_all[c * 128:(c + 1) * 128] = r1.results[c][f"ssum_out{l}"]
            lat_all[0, c * 32:(c + 1) * 32] = r1.results[c][f"lat_out{l}"][0]
        last = (l == depth - 1)
        prog = Prog(dict(n_own=n_own, layers=[l], phases=[("P2", l, last)], debug=debug))
        nc2 = prog.build()
        maps2 = []
        for c in range(n_cores):
            m = dict(maps[c])
            m[f"ssum_all{l}"] = ss_all
            m[f"lat_all{l}"] = lat_all
            maps2.append(m)
        r2 = run_bass_kernel_spmd(nc2, maps2, core_ids=list(range(n_cores)))
        hn = np.zeros_like(h)
        for c in range(n_cores):
            b, s = c // cores_per_seq, c % cores_per_seq
            hn[b, s * n_own:(s + 1) * n_own] = r2.results[c][f"h_out{l}"]
        h = hn
        if debug:
            dbg = (r1, r2)
    return h, dbg


def kernel(x, norm1_w, w_in, conv_w, conv_b, dt_bias, a_log, d_skip, ssm_norm_w,
           w_attn_branch, w_ssm_branch, w_out, norm2_w, w_ffn_in, w_ffn_out, rel_bias, final_norm_w):
    p = dict(norm1_w=norm1_w, w_in=w_in, conv_w=conv_w, conv_b=conv_b, dt_bias=dt_bias, a_log=a_log, d_skip=d_skip,
             ssm_norm_w=ssm_norm_w, w_attn_branch=w_attn_branch, w_ssm_branch=w_ssm_branch, w_out=w_out,
             norm2_w=norm2_w, w_ffn_in=w_ffn_in, w_ffn_out=w_ffn_out, rel_bias=rel_bias, final_norm_w=final_norm_w)
    p = {k: np.ascontiguousarray(np.asarray(v, dtype=np.float32)) for k, v in p.items()}
    x = np.ascontiguousarray(np.asarray(x, dtype=np.float32))
    out, _ = run_layers(x, p, 8, 4, 4096, 2)
    return out.astype(np.float32)
```
